# Optimizing a Trainium2 kernel written in Bass

```python
import jax, jax.numpy as jnp
from jax import lax
import numpy as np

D_MODEL = 2048
BATCH = 16
SEQ = 256
DEPTH = 4
DEC_BATCH = 4
DEC_SEQ = 1024
PAST_LEN = 256

GRID_W = 64
ROPE_BASE = 10000.0
EPS = 1e-6
NEG_INF = -1e30
Q_BLOCK = 128
H_A = 8
DK_A = 128
DV_A = 128
CHUNK = 32
H_B = 8
Q_LORA = 512
KV_LORA = 256
NOPE_B = 128
ROPE_B = 64
V_B = 128
H_C = 8
KVH_C = 2
HD_C = 128
WINDOW = 128
N_BRANCH = 3
D_FF = 5504
CONV_W = 3

IN_SIZES = (H_A * DK_A, H_A * DK_A, H_A * DK_A, H_A * DV_A, H_A * DV_A,
            Q_LORA, KV_LORA, ROPE_B,
            H_C * HD_C, KVH_C * HD_C, KVH_C * HD_C,
            N_BRANCH * D_MODEL)
IN_TOTAL = sum(IN_SIZES)
IN_SPLITS = tuple(sum(IN_SIZES[:i + 1]) for i in range(len(IN_SIZES) - 1))

kernel_name = "hybrid_diffusion_hgrn2_mla_swa_step"


def rms_norm(x, w):
    xf = x.astype(jnp.float32)
    y = xf * lax.rsqrt(jnp.mean(xf * xf, axis=-1, keepdims=True) + EPS)
    return y.astype(x.dtype) * w


def modulation(cvec, w_mod, b_mod):
    return (jax.nn.silu(cvec) @ w_mod + b_mod)[:, None, :]


def axial_rope(x):
    n, d = x.shape[1], x.shape[-1]
    rows = n // GRID_W
    row = jnp.repeat(jnp.arange(rows), GRID_W).astype(jnp.float32)
    col = jnp.tile(jnp.arange(GRID_W), rows).astype(jnp.float32)
    half = d // 2
    inv = ROPE_BASE ** (-jnp.arange(0, half, 2, dtype=jnp.float32) / half)

    def rot(xa, pos):
        ang = pos[:, None] * inv[None, :]
        cos = jnp.cos(ang)[:, None, :].astype(x.dtype)
        sin = jnp.sin(ang)[:, None, :].astype(x.dtype)
        x1, x2 = xa[..., :half // 2], xa[..., half // 2:]
        return jnp.concatenate([x1 * cos - x2 * sin, x1 * sin + x2 * cos], axis=-1)

    return jnp.concatenate([rot(x[..., :half], row), rot(x[..., half:], col)], axis=-1)


def softmax_with_sink(s, sink):
    if sink is None:
        return jax.nn.softmax(s, axis=-1)
    sk = sink.astype(jnp.float32)[:, :, None, None]
    m = jnp.maximum(s.max(axis=-1, keepdims=True), sk)
    p = jnp.exp(s - m)
    return p / (p.sum(axis=-1, keepdims=True) + jnp.exp(sk - m))


def dense_attention(q, k, v, sink):
    b, n, g, r, dk = q.shape
    scale = dk ** -0.5
    qb = jnp.moveaxis(q.reshape(b, n // Q_BLOCK, Q_BLOCK, g, r, dk), 1, 0)

    def one_block(qi):
        s = jnp.einsum('bqgrd,bkgd->bgrqk', qi, k).astype(jnp.float32) * scale
        p = softmax_with_sink(s, sink)
        return jnp.einsum('bgrqk,bkgd->bqgrd', p, v)

    out = lax.map(one_block, qb)
    return jnp.moveaxis(out, 0, 1).reshape(b, n, g, r, v.shape[-1])


def window_attention(q, k, v, k_ctx, v_ctx, sink):
    b, n, g, r, d = q.shape
    w = WINDOW
    nb = n // w
    scale = d ** -0.5
    pad = ((0, 0), (w, w), (0, 0), (0, 0))
    kp, vp = jnp.pad(k, pad), jnp.pad(v, pad)

    def band(a):
        return jnp.concatenate([a[:, o * w:o * w + n].reshape(b, nb, w, g, d) for o in range(3)], axis=2)

    kb, vb = band(kp), band(vp)
    qb = q.reshape(b, nb, w, g, r, d)
    s_loc = jnp.einsum('bnqgrd,bnkgd->bngrqk', qb, kb).astype(jnp.float32) * scale
    qpos = jnp.arange(nb)[:, None, None] * w + jnp.arange(w)[None, :, None]
    kpos = jnp.arange(nb)[:, None, None] * w - w + jnp.arange(3 * w)[None, None, :]
    valid = (jnp.abs(qpos - kpos) <= w) & (kpos >= 0) & (kpos < n)
    s_loc = jnp.where(valid[None, :, None, None], s_loc, NEG_INF)
    s_ctx = jnp.einsum('bnqgrd,bkgd->bngrqk', qb, k_ctx).astype(jnp.float32) * scale
    p = softmax_with_sink(jnp.concatenate([s_loc, s_ctx], axis=-1), sink)
    o = (jnp.einsum('bngrqk,bnkgd->bnqgrd', p[..., :3 * w], vb)
         + jnp.einsum('bngrqk,bkgd->bnqgrd', p[..., 3 * w:], v_ctx))
    return o.reshape(b, n, g, r, d)


def hgrn_lower_bounds(lb_param):
    cs = jnp.cumsum(jax.nn.softmax(lb_param.astype(jnp.float32), axis=0), axis=0)
    return cs - cs[0]


def forget_gate(x_f, lb, b, n):
    xf = x_f.astype(jnp.float32)
    f = lb + (1.0 - lb) * jax.nn.sigmoid(xf)
    k = (1.0 - lb) * jax.nn.sigmoid(-xf)
    return jnp.log(f).reshape(b, n, H_A, DK_A), k.reshape(b, n, H_A, DK_A)


def hgrn_scan(q, k, v, logf, s0):
    b, n, h, _ = q.shape
    nc = n // CHUNK

    def to_chunks(a):
        return jnp.moveaxis(a.astype(jnp.float32).reshape(b, nc, CHUNK, h, a.shape[-1]), 1, 0)

    mask = jnp.tril(jnp.ones((CHUNK, CHUNK), dtype=bool))

    def step(s, xs):
        qc, kc, vc, gc = xs
        g_cum = jnp.cumsum(gc, axis=1)
        o_inter = jnp.einsum('bthd,bhdv->bthv', qc * jnp.exp(g_cum), s)
        diff = g_cum[:, :, None] - g_cum[:, None, :]
        decay = jnp.where(mask[None, :, :, None, None], jnp.exp(jnp.minimum(diff, 0.0)), 0.0)
        a = jnp.einsum('bthd,bshd,btshd->bhts', qc, kc, decay)
        o_intra = jnp.einsum('bhts,bshv->bthv', a, vc)
        g_last = g_cum[:, -1]
        s_new = (jnp.exp(g_last)[..., None] * s
                 + jnp.einsum('bshd,bshv->bhdv', kc * jnp.exp(g_last[:, None] - g_cum), vc))
        return s_new, o_inter + o_intra

    s_fin, o = lax.scan(step, s0.astype(jnp.float32), (to_chunks(q), to_chunks(k), to_chunks(v), to_chunks(logf)))
    o = jnp.moveaxis(o, 0, 1).reshape(b, n, h, v.shape[-1])
    return o.astype(q.dtype), s_fin


def mla_kv(ckv, krope, w_ukv):
    b, l, _ = ckv.shape
    kv = (ckv @ w_ukv).reshape(b, l, H_B, NOPE_B + V_B)
    k = jnp.concatenate([kv[..., :NOPE_B], jnp.broadcast_to(krope[:, :, None, :], (b, l, H_B, ROPE_B))], axis=-1)
    return k, kv[..., NOPE_B:]


def token_mixer(h, lw, ctx):
    (w_in, lb, g_norm, g_q, w_uq, g_kv, w_ukv, sink, w_a, w_b, w_c, w_o) = lw
    latent = ctx is not None
    b, n, _ = h.shape
    z = h @ w_in
    (aq, af_fwd, af_bwd, ai, ag, bq, bkv, bkr, cq, ck, cv, gts) = jnp.split(z, IN_SPLITS, axis=-1)

    q_a = aq.reshape(b, n, H_A, DK_A)
    v_a = ai.reshape(b, n, H_A, DV_A)
    logf_f, k_f = forget_gate(af_fwd, lb[0], b, n)
    logf_b, k_b = forget_gate(af_bwd, lb[1], b, n)
    if latent:
        s0_f, s0_b = ctx[0][:, 0], ctx[0][:, 1]
    else:
        s0_f = jnp.zeros((b, H_A, DK_A, DV_A), jnp.float32)
        s0_b = s0_f
    o_f, s_f = hgrn_scan(q_a, k_f, v_a, logf_f, s0_f)
    o_b, s_b = hgrn_scan(q_a[:, ::-1], k_b[:, ::-1], v_a[:, ::-1], logf_b[:, ::-1], s0_b)
    o_a = rms_norm(o_f + o_b[:, ::-1], g_norm).reshape(b, n, H_A * DV_A) * jax.nn.silu(ag)

    q_b = (rms_norm(bq, g_q) @ w_uq).reshape(b, n, H_B, NOPE_B + ROPE_B)
    q_nope, q_rope = q_b[..., :NOPE_B], q_b[..., NOPE_B:]
    ckv = rms_norm(bkv, g_kv)
    krope = bkr
    if latent:
        q_rope = axial_rope(q_rope)
        k_lat, v_lat = mla_kv(ckv, axial_rope(krope[:, :, None])[:, :, 0], w_ukv)
        k_cx, v_cx = mla_kv(ctx[1], ctx[2], w_ukv)
        k_all = jnp.concatenate([k_lat, k_cx], axis=1)
        v_all = jnp.concatenate([v_lat, v_cx], axis=1)
    else:
        k_all, v_all = mla_kv(ckv, krope, w_ukv)
    q_full = jnp.concatenate([q_nope, q_rope], axis=-1)[:, :, :, None]
    o_b = dense_attention(q_full, k_all, v_all, None).reshape(b, n, H_B * V_B)

    q_c = cq.reshape(b, n, H_C, HD_C)
    k_c = ck.reshape(b, n, KVH_C, HD_C)
    v_c = cv.reshape(b, n, KVH_C, HD_C)
    sink_c = sink.reshape(KVH_C, H_C // KVH_C)
    if latent:
        q_c = axial_rope(q_c).reshape(b, n, KVH_C, H_C // KVH_C, HD_C)
        o_c = window_attention(q_c, axial_rope(k_c), v_c, ctx[3], ctx[4], sink_c)
    else:
        o_c = dense_attention(q_c.reshape(b, n, KVH_C, H_C // KVH_C, HD_C), k_c, v_c, sink_c)
    o_c = o_c.reshape(b, n, H_C * HD_C)

    gates = jax.nn.sigmoid(gts.reshape(b, n, N_BRANCH, D_MODEL))
    merged = gates[:, :, 0] * (o_a @ w_a) + gates[:, :, 1] * (o_b @ w_b) + gates[:, :, 2] * (o_c @ w_c)
    out = merged @ w_o
    new_ctx = None if latent else (jnp.stack([s_f, s_b], axis=1), ckv, krope, k_c, v_c)
    return out, new_ctx


def conv_ffn(h, w_up, w_conv, w_down):
    n = h.shape[1]
    u = h @ w_up
    half = CONV_W // 2
    up = jnp.pad(u, ((0, 0), (half, half), (0, 0)))
    u = sum(w_conv[j] * up[:, j:j + n] for j in range(CONV_W))
    a, g = jnp.split(u, 2, axis=-1)
    return (a * jax.nn.gelu(g)) @ w_down


def trunk_layer(x, mod, norms, lw, fw, ctx):
    n_pre_a, n_post_a, n_pre_f, n_post_f = norms
    shift1, scale1, gate1, shift2, scale2, gate2 = jnp.split(mod, 6, axis=-1)
    h = rms_norm(x, n_pre_a) * (1.0 + scale1) + shift1
    y, new_ctx = token_mixer(h, lw, ctx)
    x = x + gate1 * rms_norm(y, n_post_a)
    h = rms_norm(x, n_pre_f) * (1.0 + scale2) + shift2
    x = x + gate2 * rms_norm(conv_ffn(h, *fw), n_post_f)
    return x, new_ctx


def setup_inputs(seed: int = 0) -> dict:
    key = jax.random.key(seed)
    ks = iter(jax.random.split(key, 40))

    def nrm(shape, scale):
        return jax.random.normal(next(ks), shape, jnp.float32) * scale

    def gain(shape):
        return 1.0 + nrm(shape, 0.01)

    d = D_MODEL
    return {
        "x_prompt": nrm((BATCH, SEQ, d), 1.0),
        "x_sample": nrm((DEC_BATCH, DEC_SEQ, d), 1.0),
        "state_hgrn": nrm((DEC_BATCH, DEPTH, 2, H_A, DK_A, DV_A), 0.5),
        "cache_mla_ckv": nrm((DEC_BATCH, DEPTH, PAST_LEN, KV_LORA), 1.0),
        "cache_mla_krope": nrm((DEC_BATCH, DEPTH, PAST_LEN, ROPE_B), 1.0),
        "cache_swa_k": nrm((DEC_BATCH, DEPTH, PAST_LEN, KVH_C, HD_C), 1.0),
        "cache_swa_v": nrm((DEC_BATCH, DEPTH, PAST_LEN, KVH_C, HD_C), 1.0),
        "c": nrm((DEC_BATCH, d), 1.0),
        "c_ctx": nrm((d,), 1.0),
        "w_mod": nrm((DEPTH, d, 6 * d), 0.5 * d ** -0.5),
        "b_mod": nrm((DEPTH, 6 * d), 0.01),
        "norm_pre_attn": gain((DEPTH, d)),
        "norm_post_attn": gain((DEPTH, d)),
        "norm_pre_ffn": gain((DEPTH, d)),
        "norm_post_ffn": gain((DEPTH, d)),
        "w_in": nrm((DEPTH, d, IN_TOTAL), d ** -0.5),
        "hgrn_lb": nrm((DEPTH, 2, H_A * DK_A), 0.5),
        "hgrn_gnorm": gain((DEPTH, DV_A)),
        "mla_gq": gain((DEPTH, Q_LORA)),
        "mla_w_uq": nrm((DEPTH, Q_LORA, H_B * (NOPE_B + ROPE_B)), Q_LORA ** -0.5),
        "mla_gkv": gain((DEPTH, KV_LORA)),
        "mla_w_ukv": nrm((DEPTH, KV_LORA, H_B * (NOPE_B + V_B)), KV_LORA ** -0.5),
        "swa_sink": nrm((DEPTH, H_C), 1.0),
        "w_branch_a": nrm((DEPTH, H_A * DV_A, d), (H_A * DV_A) ** -0.5),
        "w_branch_b": nrm((DEPTH, H_B * V_B, d), (H_B * V_B) ** -0.5),
        "w_branch_c": nrm((DEPTH, H_C * HD_C, d), (H_C * HD_C) ** -0.5),
        "w_out": nrm((DEPTH, d, d), d ** -0.5),
        "ffn_w_up": nrm((DEPTH, d, 2 * D_FF), d ** -0.5),
        "ffn_conv": nrm((DEPTH, CONV_W, 2 * D_FF), CONV_W ** -0.5),
        "ffn_w_down": nrm((DEPTH, D_FF, d), D_FF ** -0.5),
    }


def reference(x_prompt, x_sample, state_hgrn, cache_mla_ckv, cache_mla_krope, cache_swa_k, cache_swa_v,
              c, c_ctx, w_mod, b_mod, norm_pre_attn, norm_post_attn, norm_pre_ffn, norm_post_ffn,
              w_in, hgrn_lb, hgrn_gnorm, mla_gq, mla_w_uq, mla_gkv, mla_w_ukv, swa_sink,
              w_branch_a, w_branch_b, w_branch_c, w_out, ffn_w_up, ffn_conv, ffn_w_down):
    lb_all = hgrn_lower_bounds(hgrn_lb)
    xp, xs = x_prompt, x_sample
    new_hgrn, new_ckv, new_krope, new_k, new_v = [], [], [], [], []
    for l in range(DEPTH):
        lw = (w_in[l], lb_all[l], hgrn_gnorm[l], mla_gq[l], mla_w_uq[l], mla_gkv[l], mla_w_ukv[l],
              swa_sink[l], w_branch_a[l], w_branch_b[l], w_branch_c[l], w_out[l])
        fw = (ffn_w_up[l], ffn_conv[l], ffn_w_down[l])
        norms = (norm_pre_attn[l], norm_post_attn[l], norm_pre_ffn[l], norm_post_ffn[l])
        mod_ctx = modulation(c_ctx[None, :], w_mod[l], b_mod[l])
        xp, ctx_l = trunk_layer(xp, mod_ctx, norms, lw, fw, None)
        new_hgrn.append(ctx_l[0])
        new_ckv.append(ctx_l[1])
        new_krope.append(ctx_l[2])
        new_k.append(ctx_l[3])
        new_v.append(ctx_l[4])
        mod_lat = modulation(c, w_mod[l], b_mod[l])
        cache_l = (state_hgrn[:, l], cache_mla_ckv[:, l], cache_mla_krope[:, l], cache_swa_k[:, l], cache_swa_v[:, l])
        xs, _ = trunk_layer(xs, mod_lat, norms, lw, fw, cache_l)
    return (xp, xs, jnp.stack(new_hgrn, axis=1), jnp.stack(new_ckv, axis=1), jnp.stack(new_krope, axis=1),
            jnp.stack(new_k, axis=1), jnp.stack(new_v, axis=1))
```

```python
import contextlib
import os
import numpy as np
import concourse.bass as bass
import concourse.mybir as mybir
from concourse.bass_utils import run_bass_kernel_spmd

F32 = mybir.dt.float32
BF16 = mybir.dt.bfloat16
AF = mybir.ActivationFunctionType
ALU = mybir.AluOpType

ENGS = ("tensor", "vector", "scalar", "gpsimd", "sync")

D = 2048
DEPTH = 4
T = 1024
DFF = 5504
NJ = 43
IN_TOTAL = 13632
C_AQ, C_AFF, C_AFB, C_AI, C_AG = 0, 1024, 2048, 3072, 4096
C_BQ, C_BKV, C_BKR, C_CQ, C_CK, C_CV, C_GT = 5120, 5632, 5888, 5952, 6976, 7232, 7488
EPS = 1e-6
NEG = -30000.0

SP_NRM = 0
SP_LBP = 256
SP_GN = 320
SP_GQ = 324
SP_GKV = 340
SP_SINK = 348
SP_CVEC = 380
SP_CFG = 396
SP_N = 440
LP_BMOD = 0
LP_CONV = 96
LP_N = 96 + 258

H_OFF = 0
S_OFF = 32768
S_BYTES = 88064
T2_OFF = S_OFF + S_BYTES
T2_BYTES = 24576
ARENA_BYTES = T2_OFF + T2_BYTES
GRAN = 2048
NSLOT = 4
SLOT_ELEMS = 4096


class Op:
    __slots__ = ("eng", "fn", "idx", "waits", "tick", "is_dma", "dsem", "dval", "signal", "sem_key")

    def __init__(self, eng, fn, is_dma):
        self.eng = eng
        self.fn = fn
        self.is_dma = is_dma
        self.waits = []
        self.tick = None
        self.signal = False
        self.dsem = None
        self.dval = None


class Prog:
    def __init__(self, nc, n_dma_sems=32):
        self.nc = nc
        self.ops = {e: [] for e in ENGS}
        self.last_w = {}
        self.rd_c = {}
        self.rd_d = {}
        self.n_dma_sems = n_dma_sems
        self.order = []

    def add(self, eng, fn, reads=(), writes=(), dma=False, sem_key=None):
        op = Op(eng, fn, dma)
        op.sem_key = sem_key
        op.idx = len(self.ops[eng])
        deps = {}

        def dep(d, kind):
            if d is op:
                return
            if d.eng == op.eng and not d.is_dma and not op.is_dma:
                if op.eng == "tensor":
                    return
            deps[id(d)] = d

        for r in reads:
            w = self.last_w.get(r)
            if w is not None:
                dep(w, "raw")
            if isinstance(r, tuple) and r[0] == "ps" and not dma:
                for e2, rd in self.rd_c.get(r, {}).items():
                    if e2 != eng:
                        dep(rd, "rar")
        for r in writes:
            w = self.last_w.get(r)
            if w is not None and not self.rd_c.get(r) and not self.rd_d.get(r):
                dep(w, "waw")
            for rd in self.rd_c.get(r, {}).values():
                dep(rd, "war")
            for rd in self.rd_d.get(r, ()):
                dep(rd, "war")
        for d in deps.values():
            op.waits.append(d)
            d.signal = True
        for r in reads:
            if dma:
                self.rd_d.setdefault(r, []).append(op)
            else:
                self.rd_c.setdefault(r, {})[eng] = op
        for r in writes:
            self.last_w[r] = op
            self.rd_c[r] = {}
            self.rd_d[r] = []
        self.ops[eng].append(op)
        self.order.append(op)
        return op

    def emit(self, final_waits=()):
        nc = self.nc
        with contextlib.ExitStack() as st:
            esem = {e: st.enter_context(nc.semaphore("s_" + e)) for e in ENGS}
            dsems = [st.enter_context(nc.semaphore("d%d" % i)) for i in range(self.n_dma_sems)]
            dcount = [0] * self.n_dma_sems
            prev_on_sem = [None] * self.n_dma_sems
            for e in ENGS:
                t = 0
                for op in self.ops[e]:
                    if (not op.is_dma) and op.signal:
                        t += 1
                        op.tick = t
            keyed = {}
            for op in self.order:
                if op.is_dma and op.sem_key is not None and op.sem_key not in keyed:
                    keyed[op.sem_key] = len(dsems)
                    dsems.append(st.enter_context(nc.semaphore("k%d" % len(keyed))))
                    dcount.append(0)
                    prev_on_sem.append(None)
            rr = 0
            for op in self.order:
                if op.is_dma:
                    if op.sem_key is not None:
                        k = keyed[op.sem_key]
                        dcount[k] += 16
                        op.dsem = k
                        op.dval = dcount[k]
                        continue
                    assert op.eng != "gpsimd", "gpsimd DMAs must use a dedicated semaphore"
                    k = rr % self.n_dma_sems
                    rr += 1
                    op.dsem = k
                    dcount[k] += 16
                    op.dval = dcount[k]
                    if prev_on_sem[k] is not None:
                        op.waits.append(prev_on_sem[k])
                    prev_on_sem[k] = op
            fin = list(final_waits)
            ops = self.ops
            nds = self.n_dma_sems

            nds = len(dsems)

            def run_engine(eng_name, eng):
                waited_c = {e: 0 for e in ENGS}
                waited_d = [0] * nds
                for op in ops[eng_name]:
                    for d in op.waits:
                        if d.is_dma:
                            assert eng_name != "gpsimd", "gpsimd must never wait on a DMA semaphore (HW hang)"
                            if waited_d[d.dsem] < d.dval:
                                eng.wait_ge(dsems[d.dsem], d.dval)
                                waited_d[d.dsem] = d.dval
                        else:
                            if waited_c[d.eng] < d.tick:
                                eng.wait_ge(esem[d.eng], d.tick)
                                waited_c[d.eng] = d.tick
                    ins = op.fn(eng)
                    if op.is_dma:
                        ins.then_inc(dsems[op.dsem], 16)
                    elif op.signal:
                        ins.then_inc(esem[eng_name], 1)
                if eng_name == "sync":
                    for d in fin:
                        if waited_d[d.dsem] < d.dval:
                            eng.wait_ge(dsems[d.dsem], d.dval)
                            waited_d[d.dsem] = d.dval

            with nc.Block() as block:
                @block.tensor
                def _(eng):
                    run_engine("tensor", eng)

                @block.vector
                def _(eng):
                    run_engine("vector", eng)

                @block.scalar
                def _(eng):
                    run_engine("scalar", eng)

                @block.gpsimd
                def _(eng):
                    run_engine("gpsimd", eng)

                @block.sync
                def _(eng):
                    run_engine("sync", eng)


class V:
    __slots__ = ("ap", "keys", "gen")

    def __init__(self, ap, keys, gen=None):
        self.ap = ap
        self.keys = tuple(keys)
        self.gen = gen

    def __getitem__(self, k):
        return V(self.ap[k], self.keys, self.gen)

    def re(self, s, **kw):
        return V(self.ap.rearrange(s, **kw), self.keys, self.gen)

    def bc(self, shape):
        return V(self.ap.broadcast_to(shape), self.keys, self.gen)


def _sk(x):
    if isinstance(x, V):
        return x.ap, x.keys
    return x, ()


class Builder:
    def __init__(self, depth):
        self.depth = depth
        self.nc = bass.Bass("TRN2", target_bir_lowering=False)
        self.P = Prog(self.nc)
        self.out_ops = []
        self.slot = 0
        self.psi = 0
        self.ps_reserved = set()
        self.slot_gen = [0] * NSLOT

    def _chk(self, v):
        if v.gen is not None:
            assert self.slot_gen[v.gen[0]] == v.gen[1], "stale weight ring slot"

    def mm(self, out, lhsT, rhs, start, stop, **kw):
        self._chk(lhsT)
        self._chk(rhs)
        self.P.add("tensor", lambda e: e.matmul(out.ap, lhsT=lhsT.ap, rhs=rhs.ap, start=start, stop=stop, **kw),
                   reads=lhsT.keys + rhs.keys, writes=out.keys)

    def tr(self, out, in_, ident):
        self.P.add("tensor", lambda e: e.transpose(out=out.ap, in_=in_.ap, identity=ident.ap),
                   reads=in_.keys + ident.keys, writes=out.keys)

    def act(self, out, in_, func, scale=1.0, bias=0.0):
        sa, sk = _sk(scale)
        ba, bk = _sk(bias)
        kw = {}
        if isinstance(scale, V) or scale != 1.0:
            kw["scale"] = sa
        if isinstance(bias, V) or bias != 0.0:
            kw["bias"] = ba
        self.P.add("scalar", lambda e: e.activation(out=out.ap, in_=in_.ap, func=func, **kw),
                   reads=in_.keys + sk + bk, writes=out.keys)

    def tt(self, out, in0, in1, op, eng="vector"):
        self.P.add(eng, lambda e: e.tensor_tensor(out=out.ap, in0=in0.ap, in1=in1.ap, op=op),
                   reads=in0.keys + in1.keys, writes=out.keys)

    def ts(self, out, in0, s1, s2, op0, op1=ALU.bypass, eng="vector"):
        a1, k1 = _sk(s1)
        a2, k2 = _sk(s2)
        if s2 is None:
            self.P.add(eng, lambda e: e.tensor_scalar(out=out.ap, in0=in0.ap, scalar1=a1, scalar2=None, op0=op0),
                       reads=in0.keys + k1, writes=out.keys)
        else:
            self.P.add(eng, lambda e: e.tensor_scalar(out=out.ap, in0=in0.ap, scalar1=a1, scalar2=a2, op0=op0, op1=op1),
                       reads=in0.keys + k1 + k2, writes=out.keys)

    def stt(self, out, in0, scalar, in1, op0, op1, eng="vector"):
        sa, sk = _sk(scalar)
        self.P.add(eng, lambda e: e.scalar_tensor_tensor(out=out.ap, in0=in0.ap, scalar=sa, in1=in1.ap, op0=op0, op1=op1),
                   reads=in0.keys + sk + in1.keys, writes=out.keys)

    def copy(self, out, in_, eng="vector"):
        self.P.add(eng, lambda e: e.tensor_copy(out=out.ap, in_=in_.ap), reads=in_.keys, writes=out.keys)

    def recip(self, out, in_):
        self.P.add("vector", lambda e: e.reciprocal(out=out.ap, in_=in_.ap), reads=in_.keys, writes=out.keys)

    def memset(self, out, val, eng="vector"):
        self.P.add(eng, lambda e: e.memset(out.ap, val), writes=out.keys)

    def scan(self, out, d0, d1):
        self.P.add("vector", lambda e: e.tensor_tensor_scan(out=out.ap, data0=d0.ap, data1=d1.ap, initial=0.0,
                                                            op0=ALU.mult, op1=ALU.add),
                   reads=d0.keys + d1.keys, writes=out.keys)

    def dma_in(self, out, src_ap, eng="sync", skeys=(), sem_key=None):
        return self.P.add(eng, lambda e: e.dma_start(out=out.ap, in_=src_ap), reads=tuple(skeys), writes=out.keys, dma=True,
                          sem_key=sem_key)

    def dma_out(self, dst_ap, in_, dkeys=(), final=True):
        op = self.P.add("sync", lambda e: e.dma_start(out=dst_ap, in_=in_.ap), reads=in_.keys, writes=tuple(dkeys), dma=True)
        if final:
            self.out_ops.append(op)
        return op

    def av(self, off, dt, shape):
        size = 4 if dt == F32 else 2
        n = 1
        for s in shape:
            n *= s
        nb = n * size
        assert off % 4 == 0
        ap = self.ARENA[:, off // 2:(off + nb) // 2]
        if dt == F32:
            ap = ap.bitcast(F32)
        if len(shape) == 2:
            ap = ap.rearrange("p (a b) -> p a b", b=shape[1])
        elif len(shape) == 3:
            ap = ap.rearrange("p (a b c) -> p a b c", b=shape[1], c=shape[2])
        keys = [("A", g) for g in range(off // GRAN, (off + nb - 1) // GRAN + 1)]
        return V(ap, keys)

    def ps(self, i, c0=0, c1=1024):
        keys = []
        if c0 < 512:
            keys.append(("ps", i, 0))
        if c1 > 512:
            keys.append(("ps", i, 1))
        return V(self.PS[i][:, c0:c1], keys)

    def psn(self):
        while True:
            i = self.psi % 4
            self.psi += 1
            if i not in self.ps_reserved:
                return i

    def wload(self, W, k0, kc, c0, ncol):
        assert kc * ncol <= SLOT_ELEMS
        s = self.slot % NSLOT
        self.slot += 1
        self.slot_gen[s] += 1
        dst = V(self.WR[:, s, 0:kc * ncol].rearrange("p (k n) -> p k n", n=ncol), [("w", s)], (s, self.slot_gen[s]))
        src = W.rearrange("(k p) n -> p k n", p=128)[:, k0:k0 + kc, c0:c0 + ncol]
        self.dma_in(dst, src, eng="gpsimd", sem_key=("w", s))
        return dst

    def build(self):
        nc = self.nc
        L = self.depth
        din = lambda n, s: nc.dram_tensor(n, s, F32, kind="ExternalInput").ap()
        dout = lambda n, s: nc.dram_tensor(n, s, F32, kind="ExternalOutput").ap()
        self.x0 = din("x0", [128, 16, T])
        self.smallp = din("smallp", [128, SP_N])
        self.layerp = din("layerp", [DEPTH, 128, LP_N])
        self.hinit = din("hinit", [DEPTH, 8, 128, 8, 128])
        self.ckv_ctx = din("ckv_ctx", [DEPTH, 128, 2, 256])
        self.kr_ctx = din("kr_ctx", [DEPTH, 64, 256])
        self.kc_ctx = din("kc_ctx", [DEPTH, 128, 2, 256])
        self.vc_ctx = din("vc_ctx", [DEPTH, 128, 2, 256])
        self.hm_in = din("hm", [128, 2, 128])
        self.swm_in = din("swm", [128, 2, 384])
        self.rp128_in = din("rp128", [128, 2, T])
        self.rp64_in = din("rp64", [128, 2, T])
        self.rt128_in = din("rt128", [128, 128])
        self.rt64_in = din("rt64", [128, 128])
        self.ident_in = din("ident", [128, 128])
        self.rm_in = din("rm", [128, 4])
        self.w_mod = din("w_mod", [L, D, 6 * D])
        self.w_in = din("w_in", [L, D, IN_TOTAL])
        self.w_uq = din("mla_w_uq", [L, 512, 1536])
        self.w_ukv = din("mla_w_ukv", [L, 256, 2048])
        self.w_a = din("w_branch_a", [L, 1024, D])
        self.w_b = din("w_branch_b", [L, 1024, D])
        self.w_c = din("w_branch_c", [L, 1024, D])
        self.w_out = din("w_out", [L, D, D])
        self.w_up = din("ffn_w_up", [L, D, 2 * DFF])
        self.w_down = din("ffn_w_down", [L, DFF, D])
        self.y = dout("y", [128, 16, T])
        self.st_o = dout("st_o", [DEPTH, 8, 128, 8, 128])
        self.ckv_o = dout("ckv_o", [DEPTH, 128, 2, T])
        self.kr_o = dout("kr_o", [DEPTH, 64, T])
        self.k_o = dout("k_o", [DEPTH, 128, 2, T])
        self.v_o = dout("v_o", [DEPTH, 128, 8, 256])

        with contextlib.ExitStack() as st:
            sb = lambda n, s, d=F32: st.enter_context(nc.sbuf_tensor(n, s, d))
            self.ARENA = sb("arena", [128, ARENA_BYTES // 2], BF16)
            self.WR = sb("wring", [128, NSLOT, SLOT_ELEMS], BF16)
            self.PS = [st.enter_context(nc.psum_tensor("ps%d" % i, [128, 1024], F32)) for i in range(4)]
            cst = lambda n, s, d=F32: V(sb(n, s, d)[:], [n])
            self.SP = cst("SP", [128, SP_N])
            self.LP = cst("LP", [128, LP_N])
            self.NW = cst("NW", [128, 2, 86])
            self.HM = cst("HM", [128, 2, 128], BF16)
            self.SWM = cst("SWM", [128, 2, 384], BF16)
            self.RP128 = cst("RP128", [128, 2, T])
            self.RP64 = cst("RP64", [128, 2, T])
            self.RT128 = cst("RT128", [128, 128], BF16)
            self.RT64 = cst("RT64", [128, 128], BF16)
            self.ID = cst("ID", [128, 128])
            self.IDB = cst("IDB", [128, 128], BF16)
            self.RM = cst("RM", [128, 4])
            self.ONES = cst("ONES", [128, 128], BF16)
            self.RST = cst("RST", [128, T], BF16)
            self.SC = cst("SC", [128, 16], BF16)
            self.LBE = cst("LBE", [128, 4, 16])
            self.LB = cst("LB", [128, 4, 16])
            self.OML = cst("OML", [128, 4, 16])
            self.NOML = cst("NOML", [128, 4, 16])
            self.LBS = cst("LBS", [128, 16])
            self.ESINK = cst("ESINK", [128, 32])
            self.MOD = cst("MOD", [128, 96])
            self.PRM = cst("PRM", [128, 4, 16])
            self.S32 = [cst("S32_%d" % i, [128, 128]) for i in range(4)]
            self.setup()
            for l in range(L):
                self.layer(l)
            self.P.emit(final_waits=self.out_ops)
        return nc

    def spc(self, c0, n=1):
        return self.SP[:, c0:c0 + n]

    def setup(self):
        self.dma_in(self.SP, self.smallp)
        for (dst, src, off, shp) in ((self.HM, self.hm_in, 0, [2, 128]), (self.SWM, self.swm_in, 1024, [2, 384]),
                                     (self.RT128, self.rt128_in, 4096, [128]), (self.RT64, self.rt64_in, 4608, [128])):
            stg = self.T2(off, F32, shp)
            self.dma_in(stg, src)
            self.copy(dst, stg)
        self.dma_in(self.RP128, self.rp128_in)
        self.dma_in(self.RP64, self.rp64_in)
        self.dma_in(self.ID, self.ident_in)
        self.copy(self.IDB, self.ID)
        self.dma_in(self.RM, self.rm_in)
        self.memset(self.ONES, 1.0)
        self.memset(self.RST, 1.0)
        self.memset(self.RST[:, 0:T:32], 0.0)
        self.act(self.SC, self.spc(SP_CVEC, 16), AF.Silu)
        lbp = self.spc(SP_LBP, 64).re("p (l x) -> p l x", x=16)
        self.act(self.LBE, lbp, AF.Exp)
        self.tt(self.LBS, self.LBE[:, 0, :], self.LBE[:, 1, :], ALU.add)
        self.tt(self.LBS, self.LBS, self.LBE[:, 2, :], ALU.add)
        self.tt(self.LBS, self.LBS, self.LBE[:, 3, :], ALU.add)
        self.recip(self.LBS, self.LBS)
        self.memset(self.LB[:, 0, :], 0.0)
        for l in range(1, 4):
            self.tt(self.LBE[:, l, :], self.LBE[:, l, :], self.LBS, ALU.mult)
            self.tt(self.LB[:, l, :], self.LB[:, l - 1, :], self.LBE[:, l, :], ALU.add)
        self.ts(self.NOML, self.LB, -1.0, None, ALU.add)
        self.ts(self.OML, self.NOML, -1.0, None, ALU.mult)
        self.act(self.ESINK, self.spc(SP_SINK, 32), AF.Exp)
        for c in range(16):
            self.dma_in(self.Xs(c), self.x0[:, c, :])

    def Hc(self, c):
        return self.av(H_OFF + c * 2048, BF16, [T])

    def Xs(self, c):
        return self.av(S_OFF + c * 4096, F32, [T])

    def Mc(self, c):
        return self.av(S_OFF + c * 2048, BF16, [T])

    def OBc(self, c):
        return self.av(S_OFF + 32768 + c * 2048, BF16, [T])

    def TA(self, off, dt, shape):
        assert off + (4 if dt == F32 else 2) * int(np.prod(shape)) <= 38912
        return self.av(S_OFF + 49152 + off, dt, shape)

    def T2(self, off, dt, shape):
        assert off + (4 if dt == F32 else 2) * int(np.prod(shape)) <= T2_BYTES
        return self.av(T2_OFF + off, dt, shape)

    def cfg(self, i):
        return self.spc(SP_CFG + i)

    def proj_fm(self, w, col, K, src, ps_i):
        for tt in range(2):
            for kc in range(K):
                self.mm(self.ps(ps_i, tt * 512, tt * 512 + 512), w[:, kc, col:col + 128], src(kc)[:, tt * 512:tt * 512 + 512],
                        start=(kc == 0), stop=(kc == K - 1))

    def rstd_from_ps(self, ps_v, out_v, n):
        self.ts(out_v, ps_v, 1.0 / n, EPS, ALU.mult, ALU.add)
        self.act(out_v, out_v, AF.Sqrt)
        self.recip(out_v, out_v)

    def rope(self, ps_i, r0, XB, F1, F2, out_v, RT, RP, ps_tmp):
        self.act(XB, self.ps(ps_i), AF.Copy)
        for tt in range(2):
            if os.environ.get("ROPENOMM", "0") == "1" and r0 == 0:
                continue
            self.mm(self.ps(ps_tmp, tt * 512, tt * 512 + 512), RT, XB[:, tt * 512:tt * 512 + 512], start=True, stop=True)
        nops = int(os.environ.get("ROPEOPS", "9")) if r0 == 0 else 9
        if nops >= 2:
            self.tt(F1[r0:128], self.ps(ps_i)[r0:128], RP[r0:128, 0, :], ALU.mult)
        if nops >= 3:
            self.tt(F2[r0:128], self.ps(ps_tmp)[r0:128], RP[r0:128, 1, :], ALU.mult)
        if nops >= 4:
            self.tt(out_v, F1[r0:128], F2[r0:128], ALU.add)

    def layer(self, l):
        sa = getattr(self, "stop_after", 99)
        self.dma_in(self.LP, self.layerp[l])
        self.modulation(l)
        xsrc = self.x0 if l == 0 else self.y
        if sa < 1:
            return self.dbg_dump()
        self.prenorm(0)
        if sa < 2:
            return self.dbg_dump()
        bro = os.environ.get("BRONLY", "abc")
        self.first_branch = {"a": 0, "b": 1, "c": 2}[bro[0]]
        self.hgrn(l)
        if sa < 3:
            return self.dbg_dump()
        if "a" in bro:
            self.branch(l, 0, self.w_a)
        if sa < 4:
            return self.dbg_dump()
        self.mla(l)
        if "b" in bro:
            self.branch(l, 1, self.w_b)
        if sa < 5:
            return self.dbg_dump()
        self.swa(l)
        if "c" in bro:
            self.branch(l, 2, self.w_c)
        if sa < 6:
            return self.dbg_dump()
        self.outproj(l)
        self.resid(xsrc, 1)
        if sa < 7:
            return self.dbg_dump()
        self.prenorm(2)
        self.ffn(l)
        self.resid(self.y, 3)

    def dbg_dump(self):
        if getattr(self, "stop_after", 99) < 6:
            self.dma_out(self.y[:, 0, 0:96], self.MOD)

    def modulation(self, l):
        pi = self.psn()
        import os
        for blk in range(int(os.environ.get("MODBLK", "48"))):
            w = self.wload(self.w_mod[l], 0, 16, blk * 256, 256)
            for mi in range(2):
                m = blk * 2 + mi
                nk = int(os.environ.get("MODKC", "16"))
                for kc in range(nk):
                    self.mm(self.ps(pi, m, m + 1), w[:, kc, mi * 128:mi * 128 + 128], self.SC[:, kc:kc + 1],
                            start=(kc == 0), stop=(kc == nk - 1))
        self.tt(self.MOD, self.ps(pi, 0, 96), self.LP[:, LP_BMOD:LP_BMOD + 96], ALU.add)
        nrm = lambda k: self.spc(SP_NRM + l * 64 + k * 16, 16)
        self.ts(self.PRM[:, 0, :], self.MOD[:, 16:32], 1.0, None, ALU.add)
        self.tt(self.PRM[:, 0, :], self.PRM[:, 0, :], nrm(0), ALU.mult)
        self.tt(self.PRM[:, 1, :], self.MOD[:, 32:48], nrm(1), ALU.mult)
        self.ts(self.PRM[:, 2, :], self.MOD[:, 64:80], 1.0, None, ALU.add)
        self.tt(self.PRM[:, 2, :], self.PRM[:, 2, :], nrm(2), ALU.mult)
        self.tt(self.PRM[:, 3, :], self.MOD[:, 80:96], nrm(3), ALU.mult)
        cw = self.LP[:, LP_CONV:LP_CONV + 258].re("p (j c) -> p j c", c=86)
        self.cw = cw
        for i, j in ((0, 0), (1, 2)):
            self.ts(self.NW[:, i, :], cw[:, j, :], self.cfg(1), -1.0, ALU.mult, ALU.mult)

    def prenorm(self, which):
        A = self.PRM[:, which, :]
        Bsh = self.MOD[:, 0:16] if which == 0 else self.MOD[:, 48:64]
        pi = self.psn()
        SQ = [self.T2(0, BF16, [T]), self.T2(2048, BF16, [T])]
        RSTD = self.T2(4096, F32, [T])
        TMP = [self.T2(8192, F32, [T]), self.T2(12288, F32, [T])]
        for c in range(16):
            sq = SQ[c % 2]
            self.act(sq, self.Xs(c), AF.Square)
            for tt in range(2):
                self.mm(self.ps(pi, tt * 512, tt * 512 + 512), self.ONES, sq[:, tt * 512:tt * 512 + 512], start=(c == 0), stop=(c == 15))
        self.rstd_from_ps(self.ps(pi), RSTD, D)
        for c in range(16):
            tmp = TMP[c % 2]
            self.tt(tmp, self.Xs(c), RSTD, ALU.mult)
            self.act(self.Hc(c), tmp, AF.Identity, scale=A[:, c:c + 1], bias=Bsh[:, c:c + 1])

    def resid(self, xsrc, gi):
        G = self.PRM[:, gi, :]
        RSTD = self.T2(4096, F32, [T])
        TMP = [self.T2(8192, F32, [T]), self.T2(12288, F32, [T])]
        self.rstd_from_ps(self.ps(self.ssq_ps), RSTD, D)
        self.ps_reserved.discard(self.ssq_ps)
        for c in range(16):
            tmp = TMP[c % 2]
            self.dma_in(self.Xs(c), xsrc[:, c, :], skeys=[("y", c)])
            self.tt(tmp, self.Hc(c), RSTD, ALU.mult)
            self.stt(self.Xs(c), tmp, G[:, c:c + 1], self.Xs(c), ALU.mult, ALU.add)
            self.dma_out(self.y[:, c, :], self.Xs(c), dkeys=[("y", c)], final=True)

    def branch(self, l, i, wbr):
        SIG = self.TA(0, F32, [T])
        TM = self.TA(4096, F32, [T])
        for mp in range(int(os.environ.get("BRMP", "8"))):
            wb_ = self.wload(wbr[l], 0, 8, mp * 256, 256)
            wg_ = self.wload(self.w_in[l], 0, 16, C_GT + i * D + mp * 256, 256)
            for mi in range(2):
                m = 2 * mp + mi
                p1 = self.psn()
                self.proj_fm(wb_, mi * 128, 8, self.OBc, p1)
                p2 = self.psn()
                self.proj_fm(wg_, mi * 128, 16, self.Hc, p2)
                self.act(SIG, self.ps(p2), AF.Sigmoid)
                if i == getattr(self, "first_branch", 0):
                    self.tt(self.Mc(m), self.ps(p1), SIG, ALU.mult)
                else:
                    self.tt(TM, self.ps(p1), SIG, ALU.mult)
                    self.tt(self.Mc(m), self.Mc(m), TM, ALU.add)

    def outproj(self, l):
        self.ssq_ps = self.psn()
        self.ps_reserved.add(self.ssq_ps)
        SQ = [self.T2(0, BF16, [T]), self.T2(2048, BF16, [T])]
        for mp in range(8):
            wo = self.wload(self.w_out[l], 0, 16, mp * 256, 256)
            for mi in range(2):
                m = 2 * mp + mi
                p1 = self.psn()
                self.proj_fm(wo, mi * 128, 16, self.Mc, p1)
                self.y_epilogue(p1, m, SQ[m % 2])

    def y_epilogue(self, p1, m, sq):
        self.act(self.Hc(m), self.ps(p1), AF.Copy)
        self.act(sq, self.ps(p1), AF.Square)
        for tt in range(2):
            self.mm(self.ps(self.ssq_ps, tt * 512, tt * 512 + 512), self.ONES, sq[:, tt * 512:tt * 512 + 512],
                    start=(m == 0), stop=(m == 15))

    def hgrn(self, l):
        a1 = lambda off, dt, shape: self.av(S_OFF + off, dt, shape)
        SALL = [a1(0, BF16, [32, 128]), a1(8192, BF16, [32, 128])]
        Q = a1(16384, F32, [T])
        SGM = a1(20480, F32, [T])
        Lg = a1(24576, F32, [T])
        Kk = a1(28672, F32, [T])
        G = self.TA(0, F32, [T])
        EG = self.TA(4096, F32, [T])
        KH = self.TA(8192, BF16, [T])
        KT = self.TA(12288, BF16, [T])
        KHT = [self.TA(o, BF16, [8, 128]) for o in (14336, 10240, 34816, 36864)]
        QT = [self.TA(16384, BF16, [T]), self.TA(18432, BF16, [T])]
        AM = [self.TA(20480, BF16, [8, 128]), self.TA(22528, BF16, [8, 128])]
        HI = self.TA(24576, F32, [8, 128])
        SG = self.TA(28672, BF16, [T])
        VT = self.TA(30720, BF16, [8, 256])
        RS = SGM
        T1 = Lg
        SQo = self.av(S_OFF + 28672, BF16, [T])
        carry = self.cfg(0)
        s32i = [0]

        def s32n():
            v = self.S32[s32i[0] % 4]
            s32i[0] += 1
            return v

        for hp in range(4):
            wi = self.wload(self.w_in[l], 0, 16, C_AI + hp * 256, 256)
            for half in range(2):
                pv = self.psn()
                for tb4 in range(4):
                    tb = half * 4 + tb4
                    for kc in range(16):
                        self.mm(self.ps(pv, tb4 * 256, tb4 * 256 + 256), self.Hc(kc)[:, tb * 128:tb * 128 + 128], wi[:, kc, :],
                                start=(kc == 0), stop=(kc == 15))
                self.act(VT[:, half * 4:half * 4 + 4, :], self.ps(pv).re("p (a b) -> p a b", b=256), AF.Copy)
            wq = self.wload(self.w_in[l], 0, 16, C_AQ + hp * 256, 256)
            wf = self.wload(self.w_in[l], 0, 16, C_AFF + hp * 256, 256)
            wbk = self.wload(self.w_in[l], 0, 16, C_AFB + hp * 256, 256)
            wg = self.wload(self.w_in[l], 0, 16, C_AG + hp * 256, 256)
            for hi in range(2):
                hd = 2 * hp + hi
                pq = self.psn()
                self.proj_fm(wq, hi * 128, 16, self.Hc, pq)
                self.act(Q, self.ps(pq), AF.Copy)
                pg = self.psn()
                self.proj_fm(wg, hi * 128, 16, self.Hc, pg)
                self.act(SG, self.ps(pg), AF.Silu)
                self.dma_in(HI, self.hinit[l, hd])
                for d in range(2):
                    wx = wf if d == 0 else wbk
                    lbc = self.LB[:, l, d * 8 + hd:d * 8 + hd + 1]
                    omlc = self.OML[:, l, d * 8 + hd:d * 8 + hd + 1]
                    nomlc = self.NOML[:, l, d * 8 + hd:d * 8 + hd + 1]
                    px = self.psn()
                    self.proj_fm(wx, hi * 128, 16, self.Hc, px)
                    self.act(SGM, self.ps(px), AF.Sigmoid)
                    self.act(Lg, SGM, AF.Ln, scale=omlc, bias=lbc)
                    self.ts(Kk, SGM, nomlc, omlc, ALU.mult, ALU.add)
                    self.scan(G, self.RST, Lg)
                    G3 = G.re("p (c t) -> p c t", t=32)
                    if d == 1:
                        self.tt(Lg, Lg, G, ALU.subtract)
                        self.tt(G3, Lg.re("p (c t) -> p c t", t=32), G3[:, :, 31:32].bc([128, 32, 32]), ALU.add)
                    self.act(EG, G, AF.Exp)
                    self.act(SGM, G, AF.Exp, scale=-1.0)
                    self.tt(QT[d], Q, EG, ALU.mult)
                    self.tt(KT, Kk, SGM, ALU.mult)
                    EG3 = EG.re("p (c t) -> p c t", t=32)
                    etot = EG3[:, :, 31:32] if d == 0 else EG3[:, :, 0:1]
                    self.tt(KH.re("p (c t) -> p c t", t=32), KT.re("p (c t) -> p c t", t=32), etot.bc([128, 32, 32]), ALU.mult)
                    pt = self.psn()
                    for tb in range(8):
                        self.mm(self.ps(pt, tb * 128, tb * 128 + 128), KH[:, tb * 128:tb * 128 + 128], self.IDB, start=True, stop=True)
                    for r in range(4):
                        self.act(KHT[r], self.ps(pt).re("p (a b) -> p a b", b=128), AF.Copy, scale=self.RM[:, r:r + 1])
                    pa = self.psn()
                    for tb in range(8):
                        self.mm(self.ps(pa, tb * 128, tb * 128 + 128), KT[:, tb * 128:tb * 128 + 128], QT[d][:, tb * 128:tb * 128 + 128],
                                start=True, stop=True)
                    self.tt(AM[d], self.ps(pa).re("p (a b) -> p a b", b=128), self.HM[:, d:d + 1, :].bc([128, 8, 128]), ALU.mult)
                    cur = s32n()
                    self.copy(cur, HI[:, d * 4 + (0 if d == 0 else 3), :])
                    halves = (0, 1) if d == 0 else (1, 0)
                    for hf in halves:
                        pk = [self.psn(), self.psn()]
                        for tb4 in range(4):
                            tb = hf * 4 + tb4
                            for r in range(4):
                                o = self.ps(pk[r // 2], (r % 2) * 512 + tb4 * 128, (r % 2) * 512 + tb4 * 128 + 128)
                                self.mm(o, KHT[r][:, tb, :], VT[:, tb, hi * 128:hi * 128 + 128], start=True, stop=True)
                        crange = range(hf * 16, hf * 16 + 16) if d == 0 else range(hf * 16 + 15, hf * 16 - 1, -1)
                        for c in crange:
                            tb4 = (c // 4) % 4
                            r = c % 4
                            kvp = self.ps(pk[r // 2], (r % 2) * 512 + tb4 * 128, (r % 2) * 512 + tb4 * 128 + 128)
                            self.act(SALL[d][:, c, :], cur, AF.Copy)
                            dec = EG[:, 32 * c + 31:32 * c + 32] if d == 0 else EG[:, 32 * c:32 * c + 1]
                            nxt = s32n()
                            self.stt(nxt, cur, dec, kvp, ALU.mult, ALU.add)
                            cur = nxt
                            if d == 0 and (c + 1) % 8 == 0:
                                seg = c // 8
                                self.dma_out(self.st_o[l, hd, :, seg, :], cur)
                                if c < 31:
                                    nxt = s32n()
                                    self.stt(nxt, cur, carry, HI[:, seg + 1, :], ALU.mult, ALU.add)
                                    cur = nxt
                            if d == 1 and c % 8 == 0:
                                seg = c // 8
                                self.dma_out(self.st_o[l, hd, :, 4 + seg, :], cur)
                                if c > 0:
                                    nxt = s32n()
                                    self.stt(nxt, cur, carry, HI[:, 4 + seg - 1, :], ALU.mult, ALU.add)
                                    cur = nxt
                po = self.psn()
                for tb in range(8):
                    o = self.ps(po, tb * 128, tb * 128 + 128)
                    self.mm(o, VT[:, tb, hi * 128:hi * 128 + 128], AM[0][:, tb, :], start=True, stop=False)
                    self.mm(o, VT[:, tb, hi * 128:hi * 128 + 128], AM[1][:, tb, :], start=False, stop=False)
                    for c in range(4 * tb, 4 * tb + 4):
                        for d in range(2):
                            self.mm(self.ps(po, c * 32, c * 32 + 32), SALL[d][:, c, :], QT[d][:, c * 32:c * 32 + 32],
                                    start=False, stop=(c == 4 * tb + 3 and d == 1))
                self.act(SQo, self.ps(po), AF.Square)
                pss = self.psn()
                for tt in range(2):
                    self.mm(self.ps(pss, tt * 512, tt * 512 + 512), self.ONES, SQo[:, tt * 512:tt * 512 + 512], start=True, stop=True)
                self.rstd_from_ps(self.ps(pss), RS, 128)
                self.tt(T1, self.ps(po), RS, ALU.mult)
                self.stt(self.OBc(hd), T1, self.spc(SP_GN + l), SG, ALU.mult, ALU.mult)

    def mla(self, l):
        scale = 192.0 ** -0.5
        CQ = [self.TA(c * 2048, BF16, [T]) for c in range(4)]
        CKVT = [self.TA(8192 + c * 2560, BF16, [1280]) for c in range(2)]
        KRT = self.TA(13312, BF16, [1280])
        QN = self.TA(15872, BF16, [T])
        QR = self.TA(17920, BF16, [T])
        KN = self.TA(19968, BF16, [1280])
        VB = self.TA(22528, BF16, [10, 128])
        PB = [self.TA(25088, BF16, [512]), self.TA(26112, BF16, [512])]
        REC = self.TA(27136, F32, [512])
        XB = self.TA(29184, BF16, [T])
        F1 = self.TA(31232, F32, [T])
        F2 = self.T2(16384, F32, [T])
        RSTD = self.T2(4096, F32, [T])
        SQ = [self.T2(0, BF16, [T]), self.T2(2048, BF16, [T])]
        BQ = [self.av(S_OFF + 32768 + c * 4096, F32, [T]) for c in range(4)]
        self.memset(KRT[0:64], 0.0)
        self.memset(QR[0:64], 0.0)
        pss = self.psn()
        self.ps_reserved.add(pss)
        for blk in range(2):
            w = self.wload(self.w_in[l], 0, 16, C_BQ + blk * 256, 256)
            for mi in range(2):
                c = blk * 2 + mi
                p1 = self.psn()
                self.proj_fm(w, mi * 128, 16, self.Hc, p1)
                self.act(BQ[c], self.ps(p1), AF.Copy)
                self.act(SQ[c % 2], self.ps(p1), AF.Square)
                for tt in range(2):
                    self.mm(self.ps(pss, tt * 512, tt * 512 + 512), self.ONES, SQ[c % 2][:, tt * 512:tt * 512 + 512],
                            start=(c == 0), stop=(c == 3))
        self.rstd_from_ps(self.ps(pss), RSTD, 512)
        self.ps_reserved.discard(pss)
        for c in range(4):
            self.tt(F1, BQ[c], RSTD, ALU.mult)
            self.ts(CQ[c], F1, self.spc(SP_GQ + l * 4 + c), None, ALU.mult)
        ms = int(os.environ.get("MLASTOP", "9"))
        if ms < 1:
            return
        w = self.wload(self.w_in[l], 0, 16, C_BKV, 256)
        pss = self.psn()
        self.ps_reserved.add(pss)
        BKV = [self.av(S_OFF + 32768 + c * 4096, F32, [T]) for c in range(2)]
        for c in range(2):
            p1 = self.psn()
            self.proj_fm(w, c * 128, 16, self.Hc, p1)
            self.act(BKV[c], self.ps(p1), AF.Copy)
            self.act(SQ[c % 2], self.ps(p1), AF.Square)
            for tt in range(2):
                self.mm(self.ps(pss, tt * 512, tt * 512 + 512), self.ONES, SQ[c % 2][:, tt * 512:tt * 512 + 512],
                        start=(c == 0), stop=(c == 1))
        self.rstd_from_ps(self.ps(pss), RSTD, 256)
        self.ps_reserved.discard(pss)
        for c in range(2):
            self.tt(F1, BKV[c], RSTD, ALU.mult)
            self.ts(BKV[c], F1, self.spc(SP_GKV + l * 2 + c), None, ALU.mult)
            self.dma_out(self.ckv_o[l, :, c, :], BKV[c])
            self.copy(CKVT[c][:, 0:T], BKV[c])
            stg = self.T2(20480 + c * 1024, F32, [256])
            self.dma_in(stg, self.ckv_ctx[l, :, c, :])
            self.copy(CKVT[c][:, T:1280], stg)
        if ms < 2:
            return
        w = self.wload(self.w_in[l], 0, 16, C_BKR - 64, 128)
        p1 = self.psn()
        self.proj_fm(w, 0, 16, self.Hc, p1)
        KRf = self.av(S_OFF + 32768 + 8192, F32, [T])
        self.act(KRf[64:128], self.ps(p1)[64:128], AF.Copy)
        self.dma_out(self.kr_o[l], KRf[64:128])
        p2 = self.psn()
        self.rope(p1, 64, XB, F1, F2, KRT[64:128, 0:T], self.RT64, self.RP64, p2)
        stg = self.T2(22528, F32, [256])
        self.dma_in(stg[64:128], self.kr_ctx[l])
        self.copy(KRT[64:128, T:1280], stg[64:128])
        if ms < 3:
            return
        wkv = self.wload(self.w_ukv[l], 0, 2, 0, 2048)
        wq = None
        for h in range(8):
            if h % 4 == 0:
                wq = self.wload(self.w_uq[l], 0, 4, (h // 4) * 768, 768)
            qc0 = (h % 4) * 192
            pa, pb = self.psn(), self.psn()
            for (pp, n0, n1, o0) in ((pa, 0, 512, 0), (pa, 512, 1024, 512), (pb, 1024, 1280, 0)):
                for kc in range(2):
                    self.mm(self.ps(pp, o0, o0 + n1 - n0), wkv[:, kc, 256 * h:256 * h + 128], CKVT[kc][:, n0:n1], start=(kc == 0), stop=(kc == 1))
            self.act(KN[:, 0:T], self.ps(pa), AF.Copy)
            self.act(KN[:, T:1280], self.ps(pb, 0, 256), AF.Copy)
            pa, pb = self.psn(), self.psn()
            for kb in range(10):
                o = self.ps(pa, kb * 128, kb * 128 + 128) if kb < 8 else self.ps(pb, (kb - 8) * 128, (kb - 8) * 128 + 128)
                for kc in range(2):
                    self.mm(o, CKVT[kc][:, kb * 128:kb * 128 + 128], wkv[:, kc, 256 * h + 128:256 * h + 256], start=(kc == 0), stop=(kc == 1))
            self.act(VB[:, 0:8, :], self.ps(pa).re("p (a b) -> p a b", b=128), AF.Copy)
            self.act(VB[:, 8:10, :], self.ps(pb, 0, 256).re("p (a b) -> p a b", b=128), AF.Copy)
            p1 = self.psn()
            self.proj_fm(wq, qc0, 4, lambda kc: CQ[kc], p1)
            self.act(QN, self.ps(p1), AF.Copy)
            p1 = self.psn()
            self.proj_fm(wq, qc0 + 64, 4, lambda kc: CQ[kc], p1)
            p2 = self.psn()
            self.rope(p1, 64, XB, F1, F2, QR[64:128], self.RT64, self.RP64, p2)
            for tt in range(2):
                pacc = self.psn()
                self.ps_reserved.add(pacc)
                for kb in range(10):
                    psc = self.psn()
                    half = kb % 2
                    sc = self.ps(psc, half * 512, half * 512 + 512)
                    self.mm(sc, KN[:, kb * 128:kb * 128 + 128], QN[:, tt * 512:tt * 512 + 512], start=True, stop=False)
                    self.mm(sc, KRT[:, kb * 128:kb * 128 + 128], QR[:, tt * 512:tt * 512 + 512], start=False, stop=True)
                    pb_ = PB[kb % 2]
                    for qh in range(2):
                        self.act(pb_[:, qh * 256:qh * 256 + 256], sc[:, qh * 256:qh * 256 + 256], AF.Exp, scale=scale,
                                 bias=self.cfg(2 + kb * 4 + tt * 2 + qh))
                    self.mm(self.ps(pacc, 0, 512), VB[:, kb, :], pb_, start=(kb == 0), stop=(kb == 9))
                    self.mm(self.ps(pacc, 512, 1024), self.ONES, pb_, start=(kb == 0), stop=(kb == 9))
                self.recip(REC, self.ps(pacc, 512, 1024))
                self.tt(self.OBc(h)[:, tt * 512:tt * 512 + 512], self.ps(pacc, 0, 512), REC, ALU.mult)
                self.ps_reserved.discard(pacc)

    def swa(self, l):
        scale = 128.0 ** -0.5
        QC = self.TA(0, BF16, [T])
        KC = [self.TA(2048 + g * 2560, BF16, [1280]) for g in range(2)]
        VC = self.TA(7168, BF16, [10, 256])
        KCf = self.TA(12288, F32, [T])
        VCf = [self.TA(16384, F32, [256]), self.TA(17408, F32, [256])]
        if os.environ.get("SWABUF", "0") == "1":
            XB = self.TA(34816, BF16, [T])
            F1 = self.T2(8192, F32, [T])
            F2 = self.T2(16384, F32, [T])
        else:
            XB = self.TA(18432, BF16, [T])
            F1 = self.TA(20480, F32, [T])
            F2 = self.TA(24576, F32, [T])
        PB = [self.TA(28672, BF16, [512]), self.TA(29696, BF16, [512])]
        DEN = self.TA(30720, F32, [T])
        w = self.wload(self.w_in[l], 0, 16, C_CK, 256)
        for g in range(2):
            p1 = self.psn()
            self.proj_fm(w, g * 128, 16, self.Hc, p1)
            swak = int(os.environ.get("SWAK", "9"))
            self.act(KCf, self.ps(p1), AF.Copy)
            if swak >= 1:
                self.dma_out(self.k_o[l, :, g, :], KCf)
            if swak >= 2:
                p2 = self.psn()
                if os.environ.get("ROPEC", "128") == "64":
                    self.rope(p1, 0, XB, F1, F2, KC[g][:, 0:T], self.RT64, self.RP64, p2)
                elif os.environ.get("ROPEC", "128") == "h":
                    self.rope(p1, 64, XB, F1, F2, KC[g][64:128, 0:T], self.RT128, self.RP128, p2)
                else:
                    self.rope(p1, 0, XB, F1, F2, KC[g][:, 0:T], self.RT128, self.RP128, p2)
            if swak >= 3:
                stg = self.T2(g * 1024, F32, [256])
                self.dma_in(stg, self.kc_ctx[l, :, g, :])
                self.copy(KC[g][:, T:1280], stg)
        ss = int(os.environ.get("SWASTOP", "99"))
        if ss < 1:
            return
        w = self.wload(self.w_in[l], 0, 16, C_CV, 256)
        for half in range(2):
            pv = self.psn()
            for tb4 in range(4):
                tb = half * 4 + tb4
                for kc in range(16):
                    self.mm(self.ps(pv, tb4 * 256, tb4 * 256 + 256), self.Hc(kc)[:, tb * 128:tb * 128 + 128], w[:, kc, :],
                            start=(kc == 0), stop=(kc == 15))
            self.act(VC[:, half * 4:half * 4 + 4, :], self.ps(pv).re("p (a b) -> p a b", b=256), AF.Copy)
            for tb4 in range(4):
                tb = half * 4 + tb4
                vf = VCf[tb % 2]
                self.copy(vf, self.ps(pv, tb4 * 256, tb4 * 256 + 256))
                self.dma_out(self.v_o[l, :, tb, :], vf)
        stg = self.T2(2048, F32, [2, 256])
        self.dma_in(stg, self.vc_ctx[l])
        self.copy(VC[:, 8:10, :], stg)
        if ss < 2:
            return
        wq = None
        for h in range(min(8, ss - 2)):
            if h % 2 == 0:
                wq = self.wload(self.w_in[l], 0, 16, C_CQ + (h // 2) * 256, 256)
            g = h // 4
            p1 = self.psn()
            self.proj_fm(wq, (h % 2) * 128, 16, self.Hc, p1)
            p2 = self.psn()
            self.rope(p1, 0, XB, F1, F2, QC, self.RT128, self.RP128, p2)
            po = self.psn()
            self.ps_reserved.add(po)
            pd = self.psn()
            self.ps_reserved.add(pd)
            nsc = 0
            for cb in range(2):
                for tt in range(2):
                    psc = self.psn()
                    sc = self.ps(psc, 0, 512)
                    self.mm(sc, KC[g][:, T + cb * 128:T + cb * 128 + 128], QC[:, tt * 512:tt * 512 + 512], start=True, stop=True)
                    pb_ = PB[nsc % 2]
                    nsc += 1
                    self.act(pb_, sc, AF.Exp, scale=scale, bias=self.cfg(2 + 8 * 4))
                    self.mm(self.ps(po, tt * 512, tt * 512 + 512), VC[:, 8 + cb, g * 128:g * 128 + 128], pb_, start=(cb == 0), stop=False)
                    self.mm(self.ps(pd, tt * 512, tt * 512 + 512), self.ONES, pb_, start=(cb == 0), stop=False)
            for kb in range(8):
                qb0 = max(0, kb - 1)
                qb1 = min(8, kb + 2)
                q0, q1 = qb0 * 128, qb1 * 128
                n = q1 - q0
                moff = 0 if kb > 0 else 128
                psc = self.psn()
                sc = self.ps(psc, 0, n)
                self.mm(sc, KC[g][:, kb * 128:kb * 128 + 128], QC[:, q0:q1], start=True, stop=True)
                pb_ = PB[nsc % 2]
                nsc += 1
                self.act(pb_[:, 0:n], sc, AF.Exp, scale=scale)
                self.tt(pb_[:, 0:n], pb_[:, 0:n], self.SWM[:, kb % 2, moff:moff + n], ALU.mult)
                for qb in range(qb0, qb1):
                    a, b = qb * 128, qb * 128 + 128
                    last = (qb == 3 and kb == 4) or (qb == 7 and kb == 7)
                    self.mm(self.ps(po, a, b), VC[:, kb, g * 128:g * 128 + 128], pb_[:, a - q0:b - q0], start=False, stop=last)
                    self.mm(self.ps(pd, a, b), self.ONES, pb_[:, a - q0:b - q0], start=False, stop=last)
            self.ts(DEN, self.ps(pd), self.ESINK[:, l * 8 + h:l * 8 + h + 1], None, ALU.add)
            self.recip(DEN, DEN)
            self.tt(self.OBc(h), self.ps(po), DEN, ALU.mult)
            self.ps_reserved.discard(po)
            self.ps_reserved.discard(pd)

    def ffn(self, l):
        ACTB = lambda j: self.av(S_OFF + j * 2048, BF16, [T])
        CA = self.T2(0, F32, [T])
        CG = self.T2(4096, F32, [T])
        TT_ = self.T2(8192, F32, [T])
        SIG = self.T2(12288, F32, [T])
        R = self.T2(16384, F32, [T])
        cw = self.cw

        def conv(dst, pi, widx):
            p = self.ps(pi)
            self.act(dst, p, AF.Identity, scale=cw[:, 1, widx:widx + 1])
            self.stt(dst[:, 1:T], p[:, 0:T - 1], cw[:, 0, widx:widx + 1], dst[:, 1:T], ALU.mult, ALU.add)
            self.stt(dst[:, 0:T - 1], p[:, 1:T], cw[:, 2, widx:widx + 1], dst[:, 0:T - 1], ALU.mult, ALU.add)
            self.stt(dst[:, 256:T:256], p[:, 255:T - 1:256], self.NW[:, 0, widx:widx + 1], dst[:, 256:T:256], ALU.mult, ALU.add)
            self.stt(dst[:, 255:T - 1:256], p[:, 256:T:256], self.NW[:, 1, widx:widx + 1], dst[:, 255:T - 1:256], ALU.mult, ALU.add)

        for jp in range(22):
            ncol = 256 if jp < 21 else 128
            wa = self.wload(self.w_up[l], 0, 16, jp * 256, ncol)
            wg = self.wload(self.w_up[l], 0, 16, DFF + jp * 256, ncol)
            for ji in range(ncol // 128):
                j = jp * 2 + ji
                pa = self.psn()
                self.proj_fm(wa, ji * 128, 16, self.Hc, pa)
                pg = self.psn()
                self.proj_fm(wg, ji * 128, 16, self.Hc, pg)
                conv(CA, pa, j)
                conv(CG, pg, NJ + j)
                self.act(TT_, CG, AF.Square)
                self.ts(TT_, TT_, 0.044715, 1.0, ALU.mult, ALU.add)
                self.tt(TT_, TT_, CG, ALU.mult)
                self.act(SIG, TT_, AF.Sigmoid, scale=1.5957691216057308)
                self.tt(R, CA, CG, ALU.mult)
                self.tt(ACTB(j), R, SIG, ALU.mult)
        self.ssq_ps = self.psn()
        self.ps_reserved.add(self.ssq_ps)
        SQ = [self.T2(20480, BF16, [T]), self.T2(22528, BF16, [T])]
        for mp in range(8):
            pm = [self.psn(), self.psn()]
            for jb in range(3):
                nj = 16 if jb < 2 else 11
                wd = self.wload(self.w_down[l], jb * 16, nj, mp * 256, 256)
                for mi in range(2):
                    for jj in range(nj):
                        j = jb * 16 + jj
                        for tt in range(2):
                            self.mm(self.ps(pm[mi], tt * 512, tt * 512 + 512), wd[:, jj, mi * 128:mi * 128 + 128],
                                    ACTB(j)[:, tt * 512:tt * 512 + 512], start=(j == 0), stop=(j == NJ - 1))
            for mi in range(2):
                m = 2 * mp + mi
                self.y_epilogue(pm[mi], m, SQ[m % 2])


_NC_CACHE = {}


def _get_nc(depth):
    if depth not in _NC_CACHE:
        _NC_CACHE[depth] = Builder(depth).build()
    return _NC_CACHE[depth]


def _fm(a):
    t, f = a.shape
    return np.ascontiguousarray(a.T.reshape(f // 128, 128, t).transpose(1, 0, 2))


def _rope_tables(d, sample):
    half = d // 2
    nf = half // 2
    tab = np.zeros((d, 2, T), np.float32)
    if not sample:
        tab[:, 0, :] = 1.0
        return tab
    t = np.arange(T)
    row = (t // 64).astype(np.float32)
    col = (t % 64).astype(np.float32)
    inv = (np.float32(10000.0) ** (-np.arange(0, half, 2, dtype=np.float32) / np.float32(half))).astype(np.float32)
    for i in range(d):
        pos = row if i < half else col
        f = (i % half) % nf
        ang = (pos * inv[f]).astype(np.float32)
        tab[i, 0] = np.cos(ang)
        tab[i, 1] = np.sin(ang)
    return tab


def _rot_T(d):
    half = d // 2
    q = half // 2
    R = np.zeros((d, d), np.float32)
    for base in (0, half):
        for i in range(q):
            R[base + i, base + i + q] = -1.0
            R[base + i + q, base + i] = 1.0
    return np.ascontiguousarray(R.T)


def kernel(x_prompt, x_sample, state_hgrn, cache_mla_ckv, cache_mla_krope, cache_swa_k, cache_swa_v,
           c, c_ctx, w_mod, b_mod, norm_pre_attn, norm_post_attn, norm_pre_ffn, norm_post_ffn,
           w_in, hgrn_lb, hgrn_gnorm, mla_gq, mla_w_uq, mla_gkv, mla_w_ukv, swa_sink,
           w_branch_a, w_branch_b, w_branch_c, w_out, ffn_w_up, ffn_conv, ffn_w_down, _depth=DEPTH, _only_maps=False):
    f32 = lambda a: np.ascontiguousarray(np.asarray(a, dtype=np.float32))
    x_prompt, x_sample = f32(x_prompt), f32(x_sample)
    nc = None if _only_maps else _get_nc(_depth)
    sp_base = np.zeros((128, SP_N), np.float32)
    norms = [f32(norm_pre_attn), f32(norm_post_attn), f32(norm_pre_ffn), f32(norm_post_ffn)]
    for l in range(DEPTH):
        for k in range(4):
            sp_base[:, SP_NRM + l * 64 + k * 16:SP_NRM + l * 64 + k * 16 + 16] = norms[k][l].reshape(16, 128).T
        sp_base[:, SP_LBP + l * 16:SP_LBP + l * 16 + 16] = f32(hgrn_lb)[l].reshape(16, 128).T
        sp_base[:, SP_GN + l] = f32(hgrn_gnorm)[l]
        sp_base[:, SP_GQ + l * 4:SP_GQ + l * 4 + 4] = f32(mla_gq)[l].reshape(4, 128).T
        sp_base[:, SP_GKV + l * 2:SP_GKV + l * 2 + 2] = f32(mla_gkv)[l].reshape(2, 128).T
        sp_base[:, SP_SINK + l * 8:SP_SINK + l * 8 + 8] = f32(swa_sink)[l][None, :]
    layerp = np.zeros((DEPTH, 128, LP_N), np.float32)
    for l in range(DEPTH):
        layerp[l, :, LP_BMOD:LP_BMOD + 96] = f32(b_mod)[l].reshape(96, 128).T
        cv = f32(ffn_conv)[l]
        layerp[l, :, LP_CONV:LP_CONV + 258] = cv.reshape(3, 86, 128).transpose(2, 0, 1).reshape(128, 258)
    s_idx = np.arange(128)[:, None]
    t_idx = np.arange(128)[None, :]
    same = (s_idx // 32) == (t_idx // 32)
    hm = np.stack([(same & (s_idx <= t_idx)), (same & (s_idx >= t_idx))], axis=1).astype(np.float32)
    ident = np.eye(128, dtype=np.float32)
    rm = (np.arange(128)[:, None] // 32 == np.arange(4)[None, :]).astype(np.float32)
    rt128 = _rot_T(128)
    rt64 = np.zeros((128, 128), np.float32)
    rt64[64:, 64:] = _rot_T(64)
    weights = {"w_mod": f32(w_mod), "w_in": f32(w_in), "mla_w_uq": f32(mla_w_uq), "mla_w_ukv": f32(mla_w_ukv),
               "w_branch_a": f32(w_branch_a), "w_branch_b": f32(w_branch_b), "w_branch_c": f32(w_branch_c),
               "w_out": f32(w_out), "ffn_w_up": f32(ffn_w_up), "ffn_w_down": f32(ffn_w_down)}
    if _depth != DEPTH:
        weights = {k: np.ascontiguousarray(v[:_depth]) for k, v in weights.items()}
    kj = np.arange(128)[:, None]
    qi = np.arange(128)[None, :]
    in_maps = []
    for core in range(8):
        sample = core >= 4
        sp = sp_base.copy()
        m = dict(weights)
        hinit = np.zeros((DEPTH, 8, 128, 8, 128), np.float32)
        ckv_ctx = np.zeros((DEPTH, 128, 2, 256), np.float32)
        kr_ctx = np.zeros((DEPTH, 64, 256), np.float32)
        kc_ctx = np.zeros((DEPTH, 128, 2, 256), np.float32)
        vc_ctx = np.zeros((DEPTH, 128, 2, 256), np.float32)
        swm = np.zeros((128, 2, 384), np.float32)
        if sample:
            b = core - 4
            x = x_sample[b]
            sp[:, SP_CVEC:SP_CVEC + 16] = f32(c)[b].reshape(16, 128).T
            sp[:, SP_CFG + 0] = 1.0
            sp[:, SP_CFG + 1] = 0.0
            sh = f32(state_hgrn)[b]
            for l in range(DEPTH):
                hinit[l, :, :, 0, :] = sh[l, 0]
                hinit[l, :, :, 4 + 3, :] = sh[l, 1]
                ckv_ctx[l] = f32(cache_mla_ckv)[b, l].T.reshape(2, 128, 256).transpose(1, 0, 2)
                kr_ctx[l] = f32(cache_mla_krope)[b, l].T
                kk = f32(cache_swa_k)[b, l]
                kc_ctx[l] = kk.transpose(2, 1, 0)
                vv = f32(cache_swa_v)[b, l].reshape(256, 256)
                vc_ctx[l] = vv.reshape(2, 128, 256).transpose(1, 0, 2)
            for par in range(2):
                swm[:, par, 0:128] = (kj <= qi)
                swm[:, par, 128:256] = 1.0
                swm[:, par, 256:384] = (kj >= qi)
        else:
            x = x_prompt[core * 4:core * 4 + 4].reshape(T, D)
            sp[:, SP_CVEC:SP_CVEC + 16] = f32(c_ctx).reshape(16, 128).T
            sp[:, SP_CFG + 0] = 0.0
            sp[:, SP_CFG + 1] = 1.0
            for kb in range(10):
                for qs in range(4):
                    vis = (kb < 8) and (kb // 2 == qs)
                    sp[:, SP_CFG + 2 + kb * 4 + qs] = 0.0 if vis else NEG
            swm[:, 0, 128:384] = 1.0
            swm[:, 1, 0:256] = 1.0
        m.update({"x0": _fm(x), "smallp": sp, "layerp": layerp, "hinit": hinit, "ckv_ctx": ckv_ctx, "kr_ctx": kr_ctx,
                  "kc_ctx": kc_ctx, "vc_ctx": vc_ctx, "hm": hm, "swm": swm,
                  "rp128": _rope_tables(128, sample), "rp64": np.concatenate([np.zeros((64, 2, T), np.float32), _rope_tables(64, sample)], axis=0),
                  "rt128": rt128, "rt64": rt64, "ident": ident, "rm": rm})
        in_maps.append(m)
    if _only_maps:
        return in_maps
    res = run_bass_kernel_spmd(nc, in_maps, core_ids=list(range(8)))
    R = res.results
    return _assemble(R)


def _assemble(R):
    def unfm(a):
        return np.ascontiguousarray(np.asarray(a).transpose(2, 1, 0).reshape(a.shape[2], -1))
    y_prompt = np.zeros((16, 256, D), np.float32)
    y_sample = np.zeros((4, 1024, D), np.float32)
    n_state = np.zeros((16, DEPTH, 2, 8, 128, 128), np.float32)
    n_ckv = np.zeros((16, DEPTH, 256, 256), np.float32)
    n_kr = np.zeros((16, DEPTH, 256, 64), np.float32)
    n_k = np.zeros((16, DEPTH, 256, 2, 128), np.float32)
    n_v = np.zeros((16, DEPTH, 256, 2, 128), np.float32)
    for core in range(8):
        r = R[core]
        yt = unfm(r["y"])
        if core >= 4:
            y_sample[core - 4] = yt
            continue
        y_prompt[core * 4:core * 4 + 4] = yt.reshape(4, 256, D)
        st_o = np.asarray(r["st_o"])
        for s in range(4):
            for d in range(2):
                n_state[core * 4 + s, :, d] = st_o[:, :, :, d * 4 + s, :]
        ckv = np.asarray(r["ckv_o"])
        kr = np.asarray(r["kr_o"])
        ko = np.asarray(r["k_o"])
        vo = np.asarray(r["v_o"])
        for l in range(DEPTH):
            ckv_t = ckv[l].transpose(2, 1, 0).reshape(T, 256)
            kr_t = kr[l].T
            k_t = ko[l].transpose(2, 1, 0)
            v_t = vo[l].transpose(1, 0, 2).reshape(T, 2, 128)
            for s in range(4):
                sl = slice(s * 256, s * 256 + 256)
                n_ckv[core * 4 + s, l] = ckv_t[sl]
                n_kr[core * 4 + s, l] = kr_t[sl]
                n_k[core * 4 + s, l] = k_t[sl]
                n_v[core * 4 + s, l] = v_t[sl]
    return (y_prompt, y_sample, n_state, n_ckv, n_kr, n_k, n_v)
```
